# Optimizing a Trainium2 kernel written in Bass

```python
import jax, jax.numpy as jnp
from jax import lax
import numpy as np

D_MODEL = 1024
BATCH = 8
SEQ = 2048
DEPTH = 4

GRID_W = 64
HEAD_DIM = 64
N_Q_HEADS = 8
N_KV_HEADS = 2
Q_GROUP = N_Q_HEADS // N_KV_HEADS
ATTN_DIM = N_Q_HEADS * HEAD_DIM
KV_DIM = N_KV_HEADS * HEAD_DIM
Q_BLOCK = 128
ROPE_THETA = 10000.0
ROPE_AXIS_DIM = HEAD_DIM // 2
CONV_HEADS = 8
CONV_DIM = CONV_HEADS * HEAD_DIM
IN_DIM = ATTN_DIM + 2 * KV_DIM + 3 * CONV_DIM
MIX_DIM = ATTN_DIM + CONV_DIM
POOL_WINDOWS = (2, 4, 8, 16)
POOL_GROUP = D_MODEL // len(POOL_WINDOWS)
D_FF = 2816
N_EVEN = (DEPTH + 1) // 2
N_ODD = DEPTH // 2
RMS_EPS = 1e-6

kernel_name = "hybrid_attn_shortconv_pool_convffn_encoder"


def rmsnorm(x, g):
    xf = x.astype(jnp.float32)
    y = xf * lax.rsqrt(jnp.mean(xf * xf, axis=-1, keepdims=True) + RMS_EPS)
    return (y * g.astype(jnp.float32)).astype(x.dtype)


def dwconv3(x, w):
    xp = jnp.pad(x, ((0, 0), (1, 1), (0, 0)))
    return xp[:, :-2] * w[0] + xp[:, 1:-1] * w[1] + xp[:, 2:] * w[2]


def axial_rope_tables(rows):
    row = jnp.repeat(jnp.arange(rows, dtype=jnp.float32), GRID_W)
    col = jnp.tile(jnp.arange(GRID_W, dtype=jnp.float32), rows)
    inv = ROPE_THETA ** (-jnp.arange(0, ROPE_AXIS_DIM, 2, dtype=jnp.float32) / ROPE_AXIS_DIM)
    ang_r = row[:, None] * inv[None, :]
    ang_c = col[:, None] * inv[None, :]
    return jnp.cos(ang_r), jnp.sin(ang_r), jnp.cos(ang_c), jnp.sin(ang_c)


def rotate_axis(x, cos, sin):
    c = cos[None, :, None, :].astype(x.dtype)
    s = sin[None, :, None, :].astype(x.dtype)
    x1, x2 = jnp.split(x, 2, axis=-1)
    return jnp.concatenate([x1 * c - x2 * s, x2 * c + x1 * s], axis=-1)


def apply_axial_rope(x, tables):
    cos_r, sin_r, cos_c, sin_c = tables
    x_row, x_col = jnp.split(x, 2, axis=-1)
    return jnp.concatenate([rotate_axis(x_row, cos_r, sin_r), rotate_axis(x_col, cos_c, sin_c)], axis=-1)


def gqa_block_attention(q, k, v):
    B, S = q.shape[0], q.shape[1]
    nblk = S // Q_BLOCK
    qb = q.reshape(B, nblk, Q_BLOCK, N_KV_HEADS, Q_GROUP, HEAD_DIM).transpose(1, 0, 2, 3, 4, 5)
    scale = HEAD_DIM ** -0.5

    def one_block(q_blk):
        s = jnp.einsum('bqhgd,bkhd->bhgqk', q_blk, k, preferred_element_type=jnp.float32) * scale
        p = jax.nn.softmax(s, axis=-1).astype(v.dtype)
        return jnp.einsum('bhgqk,bkhd->bqhgd', p, v)

    ob = lax.map(one_block, qb)
    return ob.transpose(1, 0, 2, 3, 4, 5).reshape(B, S, ATTN_DIM)


def attn_shortconv_mixer(h, w_in, q_g, k_g, conv_w, w_out, tables):
    B, S = h.shape[0], h.shape[1]
    proj = h @ w_in
    splits = np.cumsum([ATTN_DIM, KV_DIM, KV_DIM, CONV_DIM, CONV_DIM]).tolist()
    q, k, v, gate_b, gate_c, conv_in = jnp.split(proj, splits, axis=-1)
    q = apply_axial_rope(rmsnorm(q.reshape(B, S, N_Q_HEADS, HEAD_DIM), q_g), tables)
    k = apply_axial_rope(rmsnorm(k.reshape(B, S, N_KV_HEADS, HEAD_DIM), k_g), tables)
    v = v.reshape(B, S, N_KV_HEADS, HEAD_DIM)
    attn_out = gqa_block_attention(q, k, v)
    conv_out = gate_b * dwconv3(gate_c * conv_in, conv_w)
    return jnp.concatenate([attn_out, conv_out], axis=-1) @ w_out


def pool_mixer(h, pool_w, pool_scale):
    S = h.shape[1]
    t = jnp.arange(S)
    outs = []
    for gi, w in enumerate(POOL_WINDOWS):
        xg = h[..., gi * POOL_GROUP:(gi + 1) * POOL_GROUP]
        xf = xg.astype(jnp.float32)
        cs = jnp.pad(jnp.cumsum(xf, axis=1), ((0, 0), (1, 0), (0, 0)))
        lo = jnp.clip(t - w // 2, 0, S)
        hi = jnp.clip(t + w // 2, 0, S)
        cnt = (hi - lo).astype(jnp.float32)[None, :, None]
        mean = (cs[:, hi] - cs[:, lo]) / cnt
        pooled = (mean - xf).astype(h.dtype)
        outs.append(jnp.einsum('bsc,cd->bsd', pooled, pool_w[gi]))
    return jnp.concatenate(outs, axis=-1) * pool_scale


def conv_ffn(h, w_up, conv_w, w_down):
    up = dwconv3(h @ w_up, conv_w)
    u, gt = jnp.split(up, 2, axis=-1)
    return (jax.nn.gelu(gt, approximate=True) * u) @ w_down


def setup_inputs(seed: int = 0) -> dict:
    key = jax.random.key(seed)
    ks = jax.random.split(key, 12)
    f32 = jnp.float32
    x = jax.random.normal(ks[0], (BATCH, SEQ, D_MODEL), f32)
    norm_g = 1.0 + 0.05 * jax.random.normal(ks[1], (DEPTH, 4, D_MODEL), f32)
    mix_w_in = jax.random.normal(ks[2], (N_EVEN, D_MODEL, IN_DIM), f32) * D_MODEL ** -0.5
    q_norm_g = 1.0 + 0.05 * jax.random.normal(ks[3], (N_EVEN, HEAD_DIM), f32)
    k_norm_g = 1.0 + 0.05 * jax.random.normal(ks[4], (N_EVEN, HEAD_DIM), f32)
    sconv_w = jax.random.normal(ks[5], (N_EVEN, 3, CONV_DIM), f32) * 3 ** -0.5
    mix_w_out = jax.random.normal(ks[6], (N_EVEN, MIX_DIM, D_MODEL), f32) * MIX_DIM ** -0.5
    pool_w = jax.random.normal(ks[7], (N_ODD, len(POOL_WINDOWS), POOL_GROUP, POOL_GROUP), f32) * POOL_GROUP ** -0.5
    pool_scale = 1.0 + 0.1 * jax.random.normal(ks[8], (N_ODD, D_MODEL), f32)
    ffn_w_up = jax.random.normal(ks[9], (DEPTH, D_MODEL, 2 * D_FF), f32) * D_MODEL ** -0.5
    ffn_conv_w = jax.random.normal(ks[10], (DEPTH, 3, 2 * D_FF), f32) * 3 ** -0.5
    ffn_w_down = jax.random.normal(ks[11], (DEPTH, D_FF, D_MODEL), f32) * D_FF ** -0.5
    return {"x": x, "norm_g": norm_g, "mix_w_in": mix_w_in, "q_norm_g": q_norm_g,
            "k_norm_g": k_norm_g, "sconv_w": sconv_w, "mix_w_out": mix_w_out,
            "pool_w": pool_w, "pool_scale": pool_scale, "ffn_w_up": ffn_w_up,
            "ffn_conv_w": ffn_conv_w, "ffn_w_down": ffn_w_down}


def reference(x, norm_g, mix_w_in, q_norm_g, k_norm_g, sconv_w, mix_w_out,
              pool_w, pool_scale, ffn_w_up, ffn_conv_w, ffn_w_down):
    rows = x.shape[1] // GRID_W
    tables = axial_rope_tables(rows)
    for i in range(DEPTH):
        g = norm_g[i]
        h = rmsnorm(x, g[0])
        if i % 2 == 0:
            e = i // 2
            mix = attn_shortconv_mixer(h, mix_w_in[e], q_norm_g[e], k_norm_g[e],
                                       sconv_w[e], mix_w_out[e], tables)
        else:
            o = i // 2
            mix = pool_mixer(h, pool_w[o], pool_scale[o])
        x = x + rmsnorm(mix, g[1])
        y = conv_ffn(rmsnorm(x, g[2]), ffn_w_up[i], ffn_conv_w[i], ffn_w_down[i])
        x = x + rmsnorm(y, g[3])
    return x
```

```python
from contextlib import ExitStack

import numpy as np
import concourse.bass as bass
import concourse.mybir as mybir
from concourse.bass_utils import run_bass_kernel_spmd

F32 = mybir.dt.float32
BF16 = mybir.dt.bfloat16
AF = mybir.ActivationFunctionType
ALU = mybir.AluOpType

S = 2048
D = 1024
NCH = 8
DFF = 2816
NFC = 22
DEPTH = 4
EPS = 1e-6
NIN = 23
SB_BASE = 16512
SB_END = 229344
GR = 32

PERM64 = np.array([(d + 16) if (d % 32) < 16 else (d - 16) for d in range(64)])
POOL_W = (2, 4, 8, 16)


class PrmLayout:
    def __init__(self):
        self.n = 0
        self.off = {}

    def add(self, name, ncols):
        self.off[name] = self.n
        self.n += ncols


def make_prm_layout():
    L = PrmLayout()
    for i in range(DEPTH):
        for j in range(4):
            L.add(f"g{i}_{j}", 8)
        L.add(f"fcw{i}", 3 * 44)
    for e in range(2):
        L.add(f"gq{e}", 1)
        L.add(f"gqp{e}", 1)
        L.add(f"gk{e}", 1)
        L.add(f"gkp{e}", 1)
        L.add(f"scw{e}", 12)
        L.add(f"gqrow{e}", 64)
        L.add(f"gkrow{e}", 64)
    for o in range(2):
        L.add(f"psc{o}", 8)
    for gi in range(4):
        w = POOL_W[gi]
        L.add(f"pfl{gi}", w // 2)
        L.add(f"pfr{gi}", max(w // 2 - 1, 1))
    return L


PL = make_prm_layout()


def fm(vec, nchunk):
    return np.ascontiguousarray(vec.reshape(nchunk, 128).T)


def pack_prm(norm_g, q_norm_g, k_norm_g, sconv_w, pool_scale, ffn_conv_w):
    prm = np.zeros((128, PL.n), np.float32)
    for i in range(DEPTH):
        for j in range(4):
            o = PL.off[f"g{i}_{j}"]
            prm[:, o:o + 8] = fm(norm_g[i, j], 8)
        o = PL.off[f"fcw{i}"]
        for tap in range(3):
            prm[:, o + tap * 44:o + (tap + 1) * 44] = fm(ffn_conv_w[i, tap], 44)
    p64 = np.arange(128) % 64
    for e in range(2):
        prm[:, PL.off[f"gq{e}"]] = q_norm_g[e][p64]
        prm[:, PL.off[f"gqp{e}"]] = q_norm_g[e][PERM64[p64]]
        prm[:, PL.off[f"gk{e}"]] = k_norm_g[e][p64]
        prm[:, PL.off[f"gkp{e}"]] = k_norm_g[e][PERM64[p64]]
        o = PL.off[f"scw{e}"]
        for tap in range(3):
            prm[:, o + tap * 4:o + (tap + 1) * 4] = fm(sconv_w[e, tap], 4)
        o = PL.off[f"gqrow{e}"]
        prm[:, o:o + 64] = q_norm_g[e][None, :]
        o = PL.off[f"gkrow{e}"]
        prm[:, o:o + 64] = k_norm_g[e][None, :]
    for o_ in range(2):
        o = PL.off[f"psc{o_}"]
        prm[:, o:o + 8] = fm(pool_scale[o_], 8)
    for gi in range(4):
        w = POOL_W[gi]
        o = PL.off[f"pfl{gi}"]
        for t in range(w // 2):
            prm[:, o + t] = np.float32(w) / np.float32(t + w // 2)
        o = PL.off[f"pfr{gi}"]
        for k in range(w // 2 - 1):
            t = S - (w // 2 - 1) + k
            cnt = S - (t - w // 2)
            prm[:, o + k] = np.float32(w) / np.float32(cnt)
    return prm


def rope_tables():
    inv = 10000.0 ** (-np.arange(0, 32, 2, dtype=np.float64) / 32.0)
    t = np.arange(S)
    row = (t // 64).astype(np.float64)
    col = (t % 64).astype(np.float64)
    cosT = np.zeros((128, S), np.float32)
    sinT = np.zeros((128, S), np.float32)
    for p in range(128):
        d = p % 64
        pos = row if d < 32 else col
        ang = pos * inv[d % 16]
        cosT[p] = np.cos(ang).astype(np.float32)
        s = np.sin(ang).astype(np.float32)
        sinT[p] = -s if (d % 32) < 16 else s
    return cosT, sinT


def pack_w_in(w):
    def cols_for_heads(base, heads, perm):
        cs = []
        for h in heads:
            idx = np.arange(64)
            if perm:
                idx = PERM64
            cs.append(base + h * 64 + idx)
        return np.concatenate(cs)

    chunks = []
    for j in range(4):
        chunks.append(cols_for_heads(0, [2 * j, 2 * j + 1], False))
    for j in range(4):
        chunks.append(cols_for_heads(0, [2 * j, 2 * j + 1], True))
    chunks.append(cols_for_heads(512, [0, 1], False))
    chunks.append(cols_for_heads(512, [0, 1], True))
    for base in (768 + 512, 768 + 1024, 768):
        for c in range(4):
            chunks.append(base + c * 128 + np.arange(128))
    chunks.append(640 + np.arange(128))
    out = np.empty((NIN, 128, 1024), np.float32)
    for k, cols in enumerate(chunks):
        sub = w[:, cols]
        out[k] = sub.reshape(8, 128, 128).transpose(1, 0, 2).reshape(128, 1024)
    return out


def pack_w_out(w):
    return np.ascontiguousarray(w.reshape(8, 128, 8, 128).transpose(2, 1, 0, 3).reshape(8, 128, 1024))


def pack_w_up(w):
    u = w[:, :DFF].reshape(8, 128, NFC, 128)
    g = w[:, DFF:].reshape(8, 128, NFC, 128)
    ug = np.stack([u, g], axis=3)
    return np.ascontiguousarray(ug.transpose(2, 1, 0, 3, 4).reshape(NFC, 128, 2048))


def pack_w_dn(w):
    return np.ascontiguousarray(w.reshape(NFC, 128, 1024))


def pack_w_pool(w):
    return np.ascontiguousarray(w.reshape(4, 2, 128, 256).transpose(0, 2, 1, 3).reshape(4, 128, 512))


class View:
    __slots__ = ("ap", "space", "lo", "hi", "p0", "p1")

    def __init__(self, ap, space, lo, hi, p0, p1):
        self.ap, self.space, self.lo, self.hi, self.p0, self.p1 = ap, space, lo, hi, p0, p1


class Buf:
    def __init__(self, t, space, off, ncols, es):
        self.t, self.space, self.off, self.ncols, self.es = t, space, off, ncols, es

    def __call__(self, c0, c1, p0=0, p1=128):
        assert 0 <= c0 < c1 <= self.ncols, (c0, c1, self.ncols)
        return View(self.t[p0:p1, c0:c1], self.space, self.off + c0 * self.es, self.off + c1 * self.es, p0, p1)


class Op:
    __slots__ = ("idx", "eng", "fn", "deps", "is_dma", "dsem", "dval", "signal", "sigval")


COMPUTE = ("pe", "act", "dve", "pool")


class Prog:
    def __init__(self, nc):
        self.nc = nc
        self.ops = []
        self.sb_top = SB_BASE
        nsb = (SB_END + GR) // GR
        nps = 16384 // GR
        self.lw = {"sb": np.full((2, nsb), -1, np.int64), "ps": np.full((2, nps), -1, np.int64)}
        self.lr = {
            "sb": np.full((len(COMPUTE) + 1, 2, nsb), -1, np.int64),
            "ps": np.full((len(COMPUTE) + 1, 2, nps), -1, np.int64),
        }
        self.dma_sems = {}
        self.dma_counts = {}
        self.nbuf = 0
        self.record = False

    def alloc(self, ncols, dtype, name=None):
        es = 4 if dtype == F32 else 2
        nbytes = (ncols * es + 31) // 32 * 32
        off = self.sb_top
        self.sb_top += nbytes
        assert self.sb_top <= SB_END, f"SBUF overflow {self.sb_top}"
        self.nbuf += 1
        t = self.nc.alloc_sbuf_tensor_at(f"{name or 'b'}_{self.nbuf}", [128, ncols], dtype, offset=off)
        return Buf(t, "sb", off, ncols, es)

    def alloc_at(self, off, ncols, dtype, name=None):
        es = 4 if dtype == F32 else 2
        self.nbuf += 1
        t = self.nc.alloc_sbuf_tensor_at(f"{name or 'b'}_{self.nbuf}", [128, ncols], dtype, offset=off)
        return Buf(t, "sb", off, ncols, es)

    def add(self, eng, fn, reads=(), writes=(), dma=None):
        if self.record:
            op = Op()
            op.idx = 0
            op.deps = []
            return op
        op = Op()
        op.idx = len(self.ops)
        op.eng = eng
        op.fn = fn
        op.is_dma = dma is not None
        op.dsem = dma
        op.dval = 0
        op.signal = False
        op.sigval = 0
        if op.is_dma:
            self.dma_counts[dma] = self.dma_counts.get(dma, 0) + 16
            op.dval = self.dma_counts[dma]
        strong = set()
        weak = set()
        ei = COMPUTE.index(eng) if (eng in COMPUTE and not op.is_dma) else len(COMPUTE)

        def rng(v):
            if v.space == "ps":
                return (v.lo // 2048) * (2048 // GR), ((v.hi + 2047) // 2048) * (2048 // GR)
            return v.lo // GR, (v.hi + GR - 1) // GR

        def halves(v):
            return [h for h in range(2) if (h == 0 and v.p0 < 64) or (h == 1 and v.p1 > 64)]

        for v in reads:
            g0, g1 = rng(v)
            for h in halves(v):
                strong.update(np.unique(self.lw[v.space][h, g0:g1]).tolist())
                if v.space == "ps":
                    for k in range(len(COMPUTE) + 1):
                        if k != ei:
                            strong.update(np.unique(self.lr[v.space][k, h, g0:g1]).tolist())
        for v in writes:
            g0, g1 = rng(v)
            for h in halves(v):
                strong.update(np.unique(self.lw[v.space][h, g0:g1]).tolist())
                weak.update(np.unique(self.lr[v.space][:, h, g0:g1]).tolist())
        for v in reads:
            g0, g1 = rng(v)
            for h in halves(v):
                self.lr[v.space][ei, h, g0:g1] = op.idx
        for v in writes:
            g0, g1 = rng(v)
            for h in halves(v):
                self.lw[v.space][h, g0:g1] = op.idx
                self.lr[v.space][:, h, g0:g1] = -1
        strong.discard(-1)
        weak.discard(-1)
        weak -= strong
        deps = []
        for p in strong:
            po = self.ops[p]
            if po.is_dma or op.is_dma or po.eng != eng:
                deps.append(p)
            elif eng != "pe":
                deps.append(p)
        for p in weak:
            po = self.ops[p]
            if po.is_dma or op.is_dma or po.eng != eng:
                deps.append(p)
            elif eng != "pe":
                deps.append(p)
        for p in deps:
            self.ops[p].signal = True
        op.deps = deps
        self.ops.append(op)
        return op

    def emit(self):
        nc = self.nc
        with ExitStack() as st:
            esem = {e: st.enter_context(nc.semaphore(f"s_{e}")) for e in COMPUTE}
            dsem = {k: st.enter_context(nc.semaphore(f"d_{k}")) for k in self.dma_counts}
            cnt = {e: 0 for e in COMPUTE}
            for op in self.ops:
                if op.is_dma:
                    op.signal = True
                    op.sigval = op.dval
                elif op.signal:
                    cnt[op.eng] += 1
                    op.sigval = cnt[op.eng]
            self.sig_counts = dict(cnt)
            by_eng = {e: [] for e in ("pe", "act", "dve", "pool", "sp")}
            for op in self.ops:
                by_eng[op.eng].append(op)

            def run(engname, e):
                waited = {}
                for op in by_eng[engname]:
                    need = []
                    for p in sorted(op.deps):
                        po = self.ops[p]
                        if po.is_dma:
                            sem, key, val = dsem[po.dsem], ("d", po.dsem), po.sigval
                        else:
                            sem, key, val = esem[po.eng], ("e", po.eng), po.sigval
                        if waited.get(key, 0) >= val:
                            continue
                        waited[key] = val
                        need = [w for w in need if w[1] != key]
                        need.append((sem, key, val))
                    if op.fn is None:
                        for (sem, key, val) in need:
                            e.wait_ge(sem, val)
                        continue
                    for (sem, key, val) in need[:-1]:
                        e.wait_ge(sem, val)
                    ins = op.fn(e)
                    if need:
                        ins._wait_ge(need[-1][0], need[-1][2])
                    if op.is_dma:
                        ins.then_inc(dsem[op.dsem], 16)
                    elif op.signal:
                        ins.then_inc(esem[op.eng], 1)

            with nc.Block() as block:
                @block.tensor
                def _(e):
                    run("pe", e)

                @block.scalar
                def _(e):
                    run("act", e)

                @block.vector
                def _(e):
                    run("dve", e)

                @block.gpsimd
                def _(e):
                    run("pool", e)

                @block.sync
                def _(e):
                    run("sp", e)


class Sub:
    def __init__(self, parent, base, ncols):
        self.parent, self.base, self.ncols = parent, base, ncols

    def __call__(self, c0, c1, p0=0, p1=128):
        assert 0 <= c0 < c1 <= self.ncols
        return self.parent(self.base + c0, self.base + c1, p0, p1)


class DummyBuf:
    def __call__(self, c0, c1, p0=0, p1=128):
        return View(None, "sb", 0, 0, p0, p1)


class Builder:
    def __init__(self, layers, plan=None):
        self.layers = layers
        self.plan = plan
        self.dummy = DummyBuf()
        nc = bass.Bass("TRN2", target_bir_lowering=False)
        self.nc = nc
        P = Prog(nc)
        P.record = plan is None
        self.P = P
        dt = nc.dram_tensor
        self.d_x = dt("xT", [NCH, 128, S], F32, kind="ExternalInput").ap()
        self.d_prm = dt("prm", [128, PL.n], F32, kind="ExternalInput").ap()
        self.d_cos = dt("cosT", [128, S], F32, kind="ExternalInput").ap()
        self.d_sin = dt("sinT", [128, S], F32, kind="ExternalInput").ap()
        self.d_win = dt("w_in", [2, NIN, 128, 1024], F32, kind="ExternalInput").ap()
        self.d_wout = dt("w_out", [2, 8, 128, 1024], F32, kind="ExternalInput").ap()
        self.d_wpool = dt("w_pool", [2, 4, 128, 512], F32, kind="ExternalInput").ap()
        self.d_wup = dt("w_up", [DEPTH, NFC, 128, 2048], F32, kind="ExternalInput").ap()
        self.d_wdn = dt("w_dn", [DEPTH, NFC, 128, 1024], F32, kind="ExternalInput").ap()
        self.d_y = dt("yT", [NCH, 128, S], F32, kind="ExternalOutput").ap()
        pst = nc.alloc_psum_tensor("ps", [128, 4096], F32)
        self.ps = Buf(pst, "ps", 0, 4096, 4)
        self.xT = P.alloc(NCH * S, F32, "xT")
        self.HW = S + 2
        self.hT = P.alloc(NCH * self.HW, BF16, "hT")
        self.prm = P.alloc(PL.n, F32, "prm")
        self.ones = P.alloc(128, BF16, "ones")
        self.blk = P.alloc(128, BF16, "blk")
        self.cst = P.alloc(8, F32, "cst")
        self.sqbig = P.alloc(8 * 512, BF16, "sqb")
        self.sqb = [Sub(self.sqbig, k * 512, 512) for k in range(8)]
        self.sqi = 0
        self.tmp = [P.alloc(512, F32, "tmp") for _ in range(8)]
        self.w_init()
        self.tmi = 0
        self.phase_base = P.sb_top
        self.uid = 0
        self.build()

    def bank(self, b, c0=0, c1=512, p0=0, p1=128):
        return self.ps(b * 512 + c0, b * 512 + c1, p0, p1)

    def nsq(self):
        self.sqi = (self.sqi + 1) % len(self.sqb)
        return self.sqb[self.sqi]

    def ntmp(self):
        self.tmi = (self.tmi + 1) % len(self.tmp)
        return self.tmp[self.tmi]

    def pc(self, name, c=0):
        o = PL.off[name] + c
        return self.prm(o, o + 1)

    def xt(self, c, t0, t1):
        return self.xT(c * S + t0, c * S + t1)

    def ht(self, c, t0, t1):
        return self.hT(c * self.HW + t0 + 1, c * self.HW + t1 + 1)

    def dma_key(self, base):
        self.uid += 1
        return f"{base}{self.uid}"

    def mm(self, out, lhsT, rhs, start, stop):
        self.P.add("pe", lambda e: e.matmul(out.ap, lhsT.ap, rhs.ap, start=start, stop=stop),
                   reads=[lhsT, rhs], writes=[out])

    def act(self, out, in_, func, bias=None, scale=1.0):
        reads = [in_]
        kw = {}
        if bias is not None:
            reads.append(bias)
            kw["bias"] = bias.ap
        if isinstance(scale, View):
            reads.append(scale)
            kw["scale"] = scale.ap
        else:
            kw["scale"] = float(scale)
        self.P.add("act", lambda e: e.activation(out=out.ap, in_=in_.ap, func=func, **kw),
                   reads=reads, writes=[out])

    def tt(self, eng, out, in0, in1, op):
        self.P.add(eng, lambda e: e.tensor_tensor(out=out.ap, in0=in0.ap, in1=in1.ap, op=op),
                   reads=[in0, in1], writes=[out])

    def stt(self, eng, out, in0, scalar, in1, op0, op1):
        reads = [in0, in1]
        if isinstance(scalar, View):
            reads.append(scalar)
            sc = scalar.ap
        else:
            sc = float(scalar)
        self.P.add(eng, lambda e: e.scalar_tensor_tensor(out=out.ap, in0=in0.ap, scalar=sc, in1=in1.ap, op0=op0, op1=op1),
                   reads=reads, writes=[out])

    def ts(self, eng, out, in0, s1, op0, s2=None, op1=None):
        reads = [in0]
        a1 = s1
        if isinstance(s1, View):
            reads.append(s1)
            a1 = s1.ap
        a2 = s2
        if isinstance(s2, View):
            reads.append(s2)
            a2 = s2.ap
        if op1 is None:
            self.P.add(eng, lambda e: e.tensor_scalar(out=out.ap, in0=in0.ap, scalar1=a1, scalar2=None, op0=op0),
                       reads=reads, writes=[out])
        else:
            self.P.add(eng, lambda e: e.tensor_scalar(out=out.ap, in0=in0.ap, scalar1=a1, scalar2=a2, op0=op0, op1=op1),
                       reads=reads, writes=[out])

    def recip(self, out, in_):
        self.P.add("dve", lambda e: e.reciprocal(out=out.ap, in_=in_.ap), reads=[in_], writes=[out])

    def memset(self, eng, out, val):
        self.P.add(eng, lambda e: e.memset(out.ap, val), writes=[out])

    def copy(self, eng, out, in_):
        self.P.add(eng, lambda e: e.tensor_copy(out=out.ap, in_=in_.ap), reads=[in_], writes=[out])

    def dma_in(self, queue, out, src_ap, key):
        self.P.add(queue, lambda e: e.dma_start(out=out.ap, in_=src_ap), writes=[out], dma=key)

    def dma_out(self, queue, dst_ap, in_, key):
        return self.P.add(queue, lambda e: e.dma_start(out=dst_ap, in_=in_.ap), reads=[in_], dma=key)

    WRING_BYTES = 22528
    W_AHEAD = 6
    W_NSEM = 8

    def w_init(self):
        self.wbase = self.P.alloc(self.WRING_BYTES // 2, BF16, "wring")
        self.w_reqs = []
        self.w_next_get = 0
        self.w_next_issue = 0
        self.w_head = 0
        self.w_dead = 0
        self.w_items = {}
        self.w_bufcache = {}
        self.w_semlast = {}

    def w_get(self, src_fn, ncols):
        if self.plan is None:
            self.w_reqs.append((src_fn, ncols))
            return self.dummy
        i = self.w_next_get
        self.w_next_get += 1
        assert self.plan[i][1] == ncols
        self.w_fill()
        assert i in self.w_items, "weight ring too small for live set"
        return self.w_items[i][2]

    def w_release(self):
        if self.plan is None:
            return
        self.w_dead = self.w_next_get
        for k in [k for k in self.w_items if k < self.w_dead]:
            del self.w_items[k]
        self.w_fill()

    def w_fill(self):
        while self.w_next_issue < len(self.plan) and self.w_next_issue < self.w_next_get + self.W_AHEAD:
            j = self.w_next_issue
            src_fn, ncols = self.plan[j]
            size = ncols * 2
            lo = self.w_head
            if lo + size > self.WRING_BYTES:
                lo = 0
            hi = lo + size
            if any(not (hi <= a or lo >= b) for (a, b, _, _) in self.w_items.values()):
                break
            key = (lo, ncols)
            if key not in self.w_bufcache:
                self.w_bufcache[key] = self.P.alloc_at(self.wbase.off + lo, ncols, BF16, "w")
            buf = self.w_bufcache[key]
            semk = f"w{j % self.W_NSEM}"
            op = self.P.add("pool", (lambda sf, b, n: (lambda e: e.dma_start(out=b(0, n).ap, in_=sf())))(src_fn, buf, ncols),
                            writes=[buf(0, ncols)], dma=semk)
            prev = self.w_semlast.get(semk)
            if prev is not None and prev.idx not in op.deps:
                op.deps.append(prev.idx)
            self.w_semlast[semk] = op
            self.w_items[j] = (lo, hi, buf, op)
            self.w_head = hi
            self.w_next_issue += 1

    def build(self):
        P = self.P
        self.dma_in("sp", self.prm(0, PL.n), self.d_prm, "prm")
        self.store_out = False
        self.out_ops = []
        for tt in range(4):
            for c in range(NCH):
                self.dma_in("sp", self.xt(c, tt * 512, tt * 512 + 512), self.d_x[c][:, tt * 512:tt * 512 + 512], f"x{tt}_{c}")
        self.memset("dve", self.ones(0, 128), 1.0)
        self.memset("dve", self.blk(0, 128), 0.0)
        self.memset("dve", self.blk(0, 64, 0, 64), 1.0)
        self.memset("dve", self.blk(64, 128, 64, 128), 1.0)
        self.memset("dve", self.cst(0, 1), EPS)
        for li in self.layers:
            P.sb_top = self.phase_base
            if li % 2 == 0:
                self.attn_mixer(li)
            else:
                self.pool_mixer(li)
            P.sb_top = self.phase_base
            self.ffn(li)
        fin = P.add("sp", None)
        fin.deps = [o.idx for o in self.out_ops]
        if self.plan is not None:
            P.emit()

    def x_rstd(self, t0, n, statbank, out=None):
        for c in range(NCH):
            sq = self.nsq()
            xv = self.xt(c, t0, t0 + n)
            self.act(sq(0, n), xv, AF.Square)
            self.mm(self.bank(statbank, 0, n), self.ones(0, 128), sq(0, n), c == 0, c == NCH - 1)
        return self.rsqrt_bank(statbank, n, 1.0 / D, out=out)

    def rsqrt_bank(self, statbank, n, scale, out=None):
        ln = self.ntmp()
        self.act(ln(0, n), self.bank(statbank, 0, n), AF.Ln, bias=self.cst(0, 1), scale=scale)
        if out is not None:
            self.act(out, ln(0, n), AF.Exp, scale=-0.5)
            return None
        r = self.ntmp()
        self.act(r(0, n), ln(0, n), AF.Exp, scale=-0.5)
        return r

    def prenorm_pads(self):
        for c in range(NCH):
            self.memset("pool", self.hT(c * self.HW, c * self.HW + 1), 0.0)
            self.memset("pool", self.hT(c * self.HW + S + 1, c * self.HW + S + 2), 0.0)

    def prenorm_tile(self, gname, tt):
        t0 = tt * 512
        r = self.x_rstd(t0, 512, 6 + (tt % 2))
        for c in range(NCH):
            self.stt("dve", self.ht(c, t0, t0 + 512), self.xt(c, t0, t0 + 512),
                     self.pc(gname, c), r(0, 512), ALU.mult, ALU.mult)

    def prenorm_to_hT(self, gname):
        self.prenorm_pads()
        for tt in range(4):
            self.prenorm_tile(gname, tt)

    def post_evac(self, ybank, n, d, gname, scale=None, on_act=False):
        evv = self.ev(d * 512, d * 512 + n)
        sq = self.nsq()
        if scale is None:
            self.act(sq(0, n), self.bank(ybank, 0, n), AF.Square)
            if on_act:
                self.act(evv, self.bank(ybank, 0, n), AF.Copy, scale=self.pc(gname, d))
            else:
                self.ts("dve", evv, self.bank(ybank, 0, n), self.pc(gname, d), ALU.mult)
        else:
            self.act(sq(0, n), self.bank(ybank, 0, n), AF.Square, scale=scale)
            self.ts("dve", evv, self.bank(ybank, 0, n), self.pc(gname, d), ALU.mult, scale, ALU.mult)
        return sq

    def post_apply(self, statbank, t0, n):
        r = self.rsqrt_bank(statbank, n, 1.0 / D)
        for d in range(NCH):
            evv = self.ev(d * 512, d * 512 + n)
            self.tt("dve", evv, evv, r(0, n), ALU.mult)
            xv = self.xt(d, t0, t0 + n)
            self.tt("dve", xv, xv, evv, ALU.add)
            if self.store_out:
                self.out_ops.append(self.dma_out("sp", self.d_y[d][:, t0:t0 + n], xv, f"y{t0}_{d}"))

    def pool_mixer(self, li):
        P = self.P
        o = li // 2
        PADW = 8
        W = S + 2 * PADW
        self.ev = P.alloc(8 * 512, F32, "ev")
        hf = [P.alloc(W, F32, "hf") for _ in range(2)]
        sa = [P.alloc(W, F32, "sa")]
        sb = [P.alloc(W, F32, "sb")]
        rfull = P.alloc(S, F32, "rfull")
        for b in hf:
            self.memset("pool", b(0, PADW), 0.0)
            self.memset("pool", b(PADW + S, W), 0.0)
        for tt in range(4):
            self.x_rstd(tt * 512, 512, 6 + (tt % 2), out=rfull(tt * 512, tt * 512 + 512))
        for c in range(NCH):
            gi = c // 2
            w = POOL_W[gi]
            h = hf[c % 2]
            A = sa[0]
            B = sb[0]
            for hh in range(2):
                t0 = hh * 1024
                self.stt("dve", h(PADW + t0, PADW + t0 + 1024), self.xt(c, t0, t0 + 1024), self.pc(f"g{li}_0", c),
                         rfull(t0, t0 + 1024), ALU.mult, ALU.mult)
            def lvl(dst, src, lo, hi, s0, s1, eng):
                half = (lo + hi) // 2
                for (a, b_) in ((lo, half), (half, hi)):
                    self.tt(eng, dst(a + PADW, b_ + PADW), src(a + s0 + PADW, b_ + s0 + PADW),
                            src(a + s1 + PADW, b_ + s1 + PADW), ALU.add)
            lvl(A, h, -7, S + 7, -1, 0, "pool")
            cur = A
            if w >= 4:
                lvl(B, A, -6, S + 6, -1, 1, "dve")
                cur = B
            if w >= 8:
                lvl(A, B, -4, S + 4, -2, 2, "pool")
                cur = A
            if w >= 16:
                lvl(B, A, 0, S, -4, 4, "dve")
                cur = B
            nl = w // 2
            ol = PL.off[f"pfl{gi}"]
            self.tt("dve", cur(PADW, PADW + nl), cur(PADW, PADW + nl), self.prm(ol, ol + nl), ALU.mult)
            nr = w // 2 - 1
            if nr > 0:
                orr = PL.off[f"pfr{gi}"]
                self.tt("dve", cur(PADW + S - nr, PADW + S), cur(PADW + S - nr, PADW + S), self.prm(orr, orr + nr), ALU.mult)
            for hh in range(2):
                t0 = hh * 1024
                self.stt("dve", self.ht(c, t0, t0 + 1024), cur(PADW + t0, PADW + t0 + 1024), 1.0 / w,
                         h(PADW + t0, PADW + t0 + 1024), ALU.mult, ALU.subtract)
        yb = 0
        wp = [self.w_get((lambda o_=o, g_=gi: self.d_wpool[o_, g_]), 512) for gi in range(4)]
        for tt in range(4):
            t0 = tt * 512
            sb_ = 6 + (tt % 2)
            pend = None
            for d in range(NCH):
                gi, dd = d // 2, d % 2
                b = yb % 6
                yb += 1
                for cc in range(2):
                    self.mm(self.bank(b), wp[gi](cc * 256 + dd * 128, cc * 256 + dd * 128 + 128),
                            self.ht(2 * gi + cc, t0, t0 + 512), cc == 0, cc == 1)
                sq = self.post_evac(b, 512, d, f"g{li}_1", scale=self.pc(f"psc{o}", d))
                if pend is not None:
                    self.mm(self.bank(sb_), self.ones(0, 128), pend[0](0, 512), pend[1] == 0, False)
                pend = (sq, d)
            self.mm(self.bank(sb_), self.ones(0, 128), pend[0](0, 512), False, True)
            self.post_apply(sb_, t0, 512)
        self.w_release()
        self.prenorm_to_hT(f"g{li}_2")

    def ffn(self, li):
        P = self.P
        self.store_out = (li == self.layers[-1])
        self.ev = P.alloc(8 * 512, F32, "ev")
        actb = P.alloc(NFC * 1024, BF16, "act")
        fo = PL.off[f"fcw{li}"]

        def cw(tap, ch):
            return self.prm(fo + tap * 44 + ch, fo + tap * 44 + ch + 1)

        tiles = [(0, 342), (342, 342), (684, 340)]
        pair = 0
        pend_tail = [None]
        for hf in range(2):
            base = hf * 1024
            for c in range(NFC):
                wsl = self.w_get((lambda l_=li, c_=c: self.d_wup[l_, c_]), 2048)
                for (st_, n) in tiles:
                    t0 = base + st_
                    bu = (pair % 3) * 2
                    bg = bu + 1
                    pair += 1
                    for kc in range(NCH):
                        self.mm(self.bank(bu, 0, n + 2), wsl(kc * 256, kc * 256 + 128), self.ht(kc, t0 - 1, t0 + n + 1),
                                kc == 0, kc == NCH - 1)
                    for kc in range(NCH):
                        self.mm(self.bank(bg, 0, n + 2), wsl(kc * 256 + 128, kc * 256 + 256), self.ht(kc, t0 - 1, t0 + n + 1),
                                kc == 0, kc == NCH - 1)
                    tu = self.ntmp()
                    tg = self.ntmp()
                    gl = self.ntmp()
                    self.act(tu(0, n), self.bank(bu, 1, n + 1), AF.Copy, scale=cw(1, c))
                    self.act(tg(0, n), self.bank(bg, 1, n + 1), AF.Copy, scale=cw(1, NFC + c))
                    self.stt("dve", tu(0, n), self.bank(bu, 0, n), cw(0, c), tu(0, n), ALU.mult, ALU.add)
                    self.stt("dve", tg(0, n), self.bank(bg, 0, n), cw(0, NFC + c), tg(0, n), ALU.mult, ALU.add)
                    self.stt("dve", tu(0, n), self.bank(bu, 2, n + 2), cw(2, c), tu(0, n), ALU.mult, ALU.add)
                    self.stt("dve", tg(0, n), self.bank(bg, 2, n + 2), cw(2, NFC + c), tg(0, n), ALU.mult, ALU.add)
                    if pend_tail[0] is not None:
                        pend_tail[0]()

                    def tail(gl=gl, tg=tg, tu=tu, n=n, c=c, st_=st_):
                        self.act(gl(0, n), tg(0, n), AF.Gelu_apprx_tanh)
                        self.tt("pool", actb(c * 1024 + st_, c * 1024 + st_ + n), gl(0, n), tu(0, n), ALU.mult)
                    pend_tail[0] = tail
                self.w_release()
            if pend_tail[0] is not None:
                pend_tail[0]()
                pend_tail[0] = None
            for t2 in range(2):
                a0 = t2 * 512
                t0 = base + a0
                for c in range(NFC):
                    wsl = self.w_get((lambda l_=li, c_=c: self.d_wdn[l_, c_]), 1024)
                    for d in range(NCH):
                        self.mm(self.bank(d), wsl(d * 128, d * 128 + 128), actb(c * 1024 + a0, c * 1024 + a0 + 512),
                                c == 0, c == NFC - 1)
                    self.w_release()
                prev = None
                first = True
                for d in (7, 0, 1, 2, 3, 4, 5, 6):
                    sq = self.post_evac(d, 512, d, f"g{li}_3")
                    if prev is not None:
                        self.mm(self.bank(7), self.ones(0, 128), prev(0, 512), first, False)
                        first = False
                    prev = sq
                self.mm(self.bank(7), self.ones(0, 128), prev(0, 512), False, True)
                self.post_apply(7, t0, 512)

    def attn_mixer(self, li):
        P = self.P
        e = li // 2
        self.prenorm_to_hT(f"g{li}_0")
        qT = P.alloc(4 * S, BF16, "qT")
        self.ev = P.alloc_at(qT.off, 8 * 512, F32, "ev")
        kT = P.alloc(2 * S, BF16, "kT")
        VW = 16 * 2 * 128
        vaug = P.alloc(VW, BF16, "vaug")
        convT = P.alloc(4 * S, BF16, "convT")

        def mixT(c0, c1, p0=0, p1=128):
            if c0 < 4 * S:
                kc = c0 // S
                return self.ht(kc, c0 - kc * S, c1 - kc * S)if (p0 == 0 and p1 == 128) else \
                    self.hT(kc * self.HW + 1 + (c0 - kc * S), kc * self.HW + 1 + (c1 - kc * S), p0, p1)
            return convT(c0 - 4 * S, c1 - 4 * S, p0, p1)

        z = P.alloc(S + 2, F32, "z")
        tabs = [(P.alloc(512, F32, "cos"), P.alloc(512, F32, "sin")) for _ in range(1)]

        def win_get(k):
            return self.w_get((lambda e_=e, k_=k: self.d_win[e_, k_]), 1024)
        bset = [0]

        def nb():
            b = bset[0] % 6
            bset[0] += 1
            return b

        def proj(bank, wsl, t0, n=512):
            for kc in range(NCH):
                self.mm(self.bank(bank, 0, n), wsl(kc * 128, kc * 128 + 128), self.ht(kc, t0, t0 + n), kc == 0, kc == NCH - 1)

        oq = PL.off[f"gqrow{e}"]
        ok = PL.off[f"gkrow{e}"]
        self.P.add("dve", lambda en: en.reduce_max(out=self.cst(2, 3).ap, in_=self.prm(oq, oq + 64).ap,
                                                   axis=mybir.AxisListType.X, apply_absolute_value=True),
                   reads=[self.prm(oq, oq + 64)], writes=[self.cst(2, 3)])
        self.P.add("dve", lambda en: en.reduce_max(out=self.cst(3, 4).ap, in_=self.prm(ok, ok + 64).ap,
                                                   axis=mybir.AxisListType.X, apply_absolute_value=True),
                   reads=[self.prm(ok, ok + 64)], writes=[self.cst(3, 4)])
        self.tt("dve", self.cst(1, 2), self.cst(2, 3), self.cst(3, 4), ALU.mult)
        self.ts("dve", self.cst(1, 2), self.cst(1, 2), -8.0, ALU.mult)

        so = PL.off[f"scw{e}"]
        self.memset("pool", z(0, 1), 0.0)
        self.memset("pool", z(S + 1, S + 2), 0.0)
        for c in range(4):
            wgc = win_get(10 + c)
            wci = win_get(14 + c)
            wgb = win_get(18 + c)
            for tt in range(4):
                t0 = tt * 512
                b1, b2 = nb(), nb()
                proj(b1, wgc, t0)
                proj(b2, wci, t0)
                g = self.ntmp()
                self.act(g(0, 512), self.bank(b1), AF.Copy)
                self.tt("dve", z(1 + t0, 1 + t0 + 512), self.bank(b2), g(0, 512), ALU.mult)
            for tt in range(4):
                t0 = tt * 512
                b3 = nb()
                proj(b3, wgb, t0)
                t = self.ntmp()
                self.act(t(0, 512), z(1 + t0, 1 + t0 + 512), AF.Copy, scale=self.prm(so + 4 + c, so + 4 + c + 1))
                self.stt("dve", t(0, 512), z(t0, t0 + 512), self.prm(so + c, so + c + 1), t(0, 512), ALU.mult, ALU.add)
                self.stt("dve", t(0, 512), z(t0 + 2, t0 + 514), self.prm(so + 8 + c, so + 8 + c + 1), t(0, 512), ALU.mult, ALU.add)
                self.tt("dve", mixT((4 + c) * S + t0, (4 + c) * S + t0 + 512), self.bank(b3), t(0, 512), ALU.mult)
            self.w_release()
        self.memset("pool", vaug(0, VW), 1.0)
        wv = win_get(22)
        for tk in range(16):
            b = nb()
            for kc in range(NCH):
                self.mm(self.bank(b, 0, 128), self.ht(kc, tk * 128, tk * 128 + 128),
                        wv(kc * 128, kc * 128 + 128), kc == 0, kc == NCH - 1)
            self.act(vaug((tk * 2) * 128, (tk * 2) * 128 + 64), self.bank(b, 0, 64), AF.Copy)
            self.act(vaug((tk * 2 + 1) * 128, (tk * 2 + 1) * 128 + 64), self.bank(b, 64, 128), AF.Copy)
        self.w_release()
        tabi = [0]

        def qk_chunk(wmain, wperm, gname, gpname, dst, dchunk):
            for tt in range(4):
                t0 = tt * 512
                ct, stb = tabs[0]
                tabi[0] += 1
                self.dma_in("sp", ct(0, 512), self.d_cos[:, t0:t0 + 512], "cos0")
                self.dma_in("sp", stb(0, 512), self.d_sin[:, t0:t0 + 512], "sin0")
                b1, b2, b3 = nb(), nb(), nb()
                proj(b1, wmain, t0)
                proj(b2, wperm, t0)
                sq = self.nsq()
                self.act(sq(0, 512), self.bank(b1), AF.Square)
                self.mm(self.bank(b3), self.blk(0, 128), sq(0, 512), True, True)
                t1 = self.ntmp()
                t2 = self.ntmp()
                self.act(t1(0, 512), self.bank(b1), AF.Copy, scale=self.pc(gname))
                self.act(t2(0, 512), self.bank(b2), AF.Copy, scale=self.pc(gpname))
                r = self.rsqrt_bank(b3, 512, 1.0 / 64)
                self.tt("dve", t1(0, 512), t1(0, 512), ct(0, 512), ALU.mult)
                self.tt("pool", t2(0, 512), t2(0, 512), stb(0, 512), ALU.mult)
                self.tt("pool", t1(0, 512), t1(0, 512), t2(0, 512), ALU.add)
                if dst is not None:
                    self.tt("dve", dst(dchunk * S + t0, dchunk * S + t0 + 512), t1(0, 512), r(0, 512), ALU.mult)
                else:
                    for g in range(2):
                        for oh in range(2):
                            self.tt("dve", kT(g * S + t0, g * S + t0 + 512, oh * 64, oh * 64 + 64),
                                    t1(0, 512, g * 64, g * 64 + 64), r(0, 512, g * 64, g * 64 + 64), ALU.mult)

        wm = win_get(8)
        wpm = win_get(9)
        qk_chunk(wm, wpm, f"gk{e}", f"gkp{e}", None, 0)
        self.w_release()
        for j in range(4):
            wm = win_get(j)
            wpm = win_get(4 + j)
            qk_chunk(wm, wpm, f"gq{e}", f"gqp{e}", qT, j)
            self.w_release()

        steps = [(j, qt, kc) for j in range(4) for qt in range(4) for kc in range(16)]
        LAG = 1
        ptbig = [Sub(self.sqbig, k * 1024, 1024) for k in range(3)]

        def qk_step(i):
            j, qt, kc = steps[i]
            g = j // 2
            sb0 = (i % 2) * 2
            for hh in range(2):
                p0, p1 = hh * 64, hh * 64 + 64
                self.mm(self.bank(sb0 + hh), kT(g * S + kc * 128, g * S + kc * 128 + 128, p0, p1),
                        qT(j * S + qt * 512, j * S + qt * 512 + 512, p0, p1), True, True)
            pt = ptbig[i % 3]
            self.act(pt(0, 1024), self.ps(sb0 * 512, sb0 * 512 + 1024), AF.Exp, bias=self.cst(1, 2), scale=0.125)

        def pv_step(i):
            j, qt, kc = steps[i]
            g = j // 2
            pv0 = 4 + 2 * ((i // 16) % 2)
            pt = ptbig[i % 3]
            for hh in range(2):
                self.mm(self.bank(pv0 + hh), vaug((kc * 2 + g) * 128, (kc * 2 + g) * 128 + 128),
                        pt(hh * 512, hh * 512 + 512), kc == 0, kc == 15)
            if kc == 15:
                for hh in range(2):
                    rc = self.ntmp()
                    self.recip(rc(0, 512, 64, 128), self.bank(pv0 + hh, 0, 512, 64, 128))
                    self.tt("dve", mixT(j * S + qt * 512, j * S + qt * 512 + 512, hh * 64, hh * 64 + 64),
                            self.bank(pv0 + hh, 0, 512, 0, 64), rc(0, 512, 64, 128), ALU.mult)

        for i in range(len(steps) + LAG):
            if i < len(steps):
                qk_step(i)
            if i >= LAG:
                pv_step(i - LAG)

        yb = 0
        for tt in range(4):
            t0 = tt * 512
            sb_ = 6 + (tt % 2)
            pend = None
            for d in range(NCH):
                wsl = self.w_get((lambda e_=e, d_=d: self.d_wout[e_, d_]), 1024)
                b = yb % 6
                yb += 1
                for kc in range(NCH):
                    self.mm(self.bank(b), wsl(kc * 128, kc * 128 + 128), mixT(kc * S + t0, kc * S + t0 + 512),
                            kc == 0, kc == NCH - 1)
                self.w_release()
                sq = self.post_evac(b, 512, d, f"g{li}_1", on_act=True)
                if pend is not None:
                    self.mm(self.bank(sb_), self.ones(0, 128), pend[0](0, 512), pend[1] == 0, False)
                pend = (sq, d)
            self.mm(self.bank(sb_), self.ones(0, 128), pend[0](0, 512), False, True)
            self.post_apply(sb_, t0, 512)
        self.prenorm_to_hT(f"g{li}_2")


_CACHE = {}


def get_builder(layers):
    key = tuple(layers)
    if key not in _CACHE:
        rec = Builder(list(layers), plan=None)
        _CACHE[key] = Builder(list(layers), plan=rec.w_reqs)
    return _CACHE[key]


def prep_shared(norm_g, mix_w_in, q_norm_g, k_norm_g, sconv_w, mix_w_out, pool_w, pool_scale,
                ffn_w_up, ffn_conv_w, ffn_w_down):
    cosT, sinT = rope_tables()
    return {
        "prm": pack_prm(norm_g, q_norm_g, k_norm_g, sconv_w, pool_scale, ffn_conv_w),
        "cosT": cosT,
        "sinT": sinT,
        "w_in": np.stack([pack_w_in(mix_w_in[e]) for e in range(2)]),
        "w_out": np.stack([pack_w_out(mix_w_out[e]) for e in range(2)]),
        "w_pool": np.stack([pack_w_pool(pool_w[o]) for o in range(2)]),
        "w_up": np.stack([pack_w_up(ffn_w_up[i]) for i in range(DEPTH)]),
        "w_dn": np.stack([pack_w_dn(ffn_w_down[i]) for i in range(DEPTH)]),
    }


def run_layers(xT_all, shared, layers):
    b = get_builder(layers)
    in_maps = []
    for core in range(8):
        m = dict(shared)
        m["xT"] = np.ascontiguousarray(xT_all[core])
        in_maps.append(m)
    res = run_bass_kernel_spmd(b.nc, in_maps, core_ids=list(range(8)))
    return np.stack([r["yT"] for r in res.results])


LAUNCH_GROUPS = [[0, 1, 2, 3]]


def kernel(x, norm_g, mix_w_in, q_norm_g, k_norm_g, sconv_w, mix_w_out, pool_w, pool_scale,
           ffn_w_up, ffn_conv_w, ffn_w_down):
    f = lambda a: np.asarray(a, dtype=np.float32)
    x = f(x)
    shared = prep_shared(f(norm_g), f(mix_w_in), f(q_norm_g), f(k_norm_g), f(sconv_w), f(mix_w_out),
                         f(pool_w), f(pool_scale), f(ffn_w_up), f(ffn_conv_w), f(ffn_w_down))
    xT = np.ascontiguousarray(x.transpose(0, 2, 1).reshape(8, NCH, 128, S))
    for grp in LAUNCH_GROUPS:
        xT = run_layers(xT, shared, grp)
    out = xT.reshape(8, D, S).transpose(0, 2, 1)
    return np.ascontiguousarray(out).astype(np.float32)
```

```python
from contextlib import ExitStack

import numpy as np
import concourse.bass as bass
import concourse.mybir as mybir
from concourse.bass_utils import run_bass_kernel_spmd

F32 = mybir.dt.float32
BF16 = mybir.dt.bfloat16
AF = mybir.ActivationFunctionType
ALU = mybir.AluOpType

S = 2048
D = 1024
NCH = 8
DFF = 2816
NFC = 22
DEPTH = 4
EPS = 1e-6
NIN = 23
SB_BASE = 16512
SB_END = 229344
GR = 32

PERM64 = np.array([(d + 16) if (d % 32) < 16 else (d - 16) for d in range(64)])
POOL_W = (2, 4, 8, 16)


class PrmLayout:
    def __init__(self):
        self.n = 0
        self.off = {}

    def add(self, name, ncols):
        self.off[name] = self.n
        self.n += ncols


def make_prm_layout():
    L = PrmLayout()
    for i in range(DEPTH):
        for j in range(4):
            L.add(f"g{i}_{j}", 8)
        L.add(f"fcw{i}", 3 * 44)
    for e in range(2):
        L.add(f"gq{e}", 1)
        L.add(f"gqp{e}", 1)
        L.add(f"gk{e}", 1)
        L.add(f"gkp{e}", 1)
        L.add(f"scw{e}", 12)
        L.add(f"gqrow{e}", 64)
        L.add(f"gkrow{e}", 64)
    for o in range(2):
        L.add(f"psc{o}", 8)
    for gi in range(4):
        w = POOL_W[gi]
        L.add(f"pfl{gi}", w // 2)
        L.add(f"pfr{gi}", max(w // 2 - 1, 1))
    return L


PL = make_prm_layout()


def fm(vec, nchunk):
    return np.ascontiguousarray(vec.reshape(nchunk, 128).T)


def pack_prm(norm_g, q_norm_g, k_norm_g, sconv_w, pool_scale, ffn_conv_w):
    prm = np.zeros((128, PL.n), np.float32)
    for i in range(DEPTH):
        for j in range(4):
            o = PL.off[f"g{i}_{j}"]
            prm[:, o:o + 8] = fm(norm_g[i, j], 8)
        o = PL.off[f"fcw{i}"]
        for tap in range(3):
            prm[:, o + tap * 44:o + (tap + 1) * 44] = fm(ffn_conv_w[i, tap], 44)
    p64 = np.arange(128) % 64
    for e in range(2):
        prm[:, PL.off[f"gq{e}"]] = q_norm_g[e][p64]
        prm[:, PL.off[f"gqp{e}"]] = q_norm_g[e][PERM64[p64]]
        prm[:, PL.off[f"gk{e}"]] = k_norm_g[e][p64]
        prm[:, PL.off[f"gkp{e}"]] = k_norm_g[e][PERM64[p64]]
        o = PL.off[f"scw{e}"]
        for tap in range(3):
            prm[:, o + tap * 4:o + (tap + 1) * 4] = fm(sconv_w[e, tap], 4)
        o = PL.off[f"gqrow{e}"]
        prm[:, o:o + 64] = q_norm_g[e][None, :]
        o = PL.off[f"gkrow{e}"]
        prm[:, o:o + 64] = k_norm_g[e][None, :]
    for o_ in range(2):
        o = PL.off[f"psc{o_}"]
        prm[:, o:o + 8] = fm(pool_scale[o_], 8)
    for gi in range(4):
        w = POOL_W[gi]
        o = PL.off[f"pfl{gi}"]
        for t in range(w // 2):
            prm[:, o + t] = np.float32(w) / np.float32(t + w // 2)
        o = PL.off[f"pfr{gi}"]
        for k in range(w // 2 - 1):
            t = S - (w // 2 - 1) + k
            cnt = S - (t - w // 2)
            prm[:, o + k] = np.float32(w) / np.float32(cnt)
    return prm


def rope_tables():
    inv = 10000.0 ** (-np.arange(0, 32, 2, dtype=np.float64) / 32.0)
    t = np.arange(S)
    row = (t // 64).astype(np.float64)
    col = (t % 64).astype(np.float64)
    cosT = np.zeros((128, S), np.float32)
    sinT = np.zeros((128, S), np.float32)
    for p in range(128):
        d = p % 64
        pos = row if d < 32 else col
        ang = pos * inv[d % 16]
        cosT[p] = np.cos(ang).astype(np.float32)
        s = np.sin(ang).astype(np.float32)
        sinT[p] = -s if (d % 32) < 16 else s
    return cosT, sinT


def pack_w_in(w):
    def cols_for_heads(base, heads, perm):
        cs = []
        for h in heads:
            idx = np.arange(64)
            if perm:
                idx = PERM64
            cs.append(base + h * 64 + idx)
        return np.concatenate(cs)

    chunks = []
    for j in range(4):
        chunks.append(cols_for_heads(0, [2 * j, 2 * j + 1], False))
    for j in range(4):
        chunks.append(cols_for_heads(0, [2 * j, 2 * j + 1], True))
    chunks.append(cols_for_heads(512, [0, 1], False))
    chunks.append(cols_for_heads(512, [0, 1], True))
    for base in (768 + 512, 768 + 1024, 768):
        for c in range(4):
            chunks.append(base + c * 128 + np.arange(128))
    chunks.append(640 + np.arange(128))
    out = np.empty((NIN, 128, 1024), np.float32)
    for k, cols in enumerate(chunks):
        sub = w[:, cols]
        out[k] = sub.reshape(8, 128, 128).transpose(1, 0, 2).reshape(128, 1024)
    return out


def pack_w_out(w):
    return np.ascontiguousarray(w.reshape(8, 128, 8, 128).transpose(2, 1, 0, 3).reshape(8, 128, 1024))


def pack_w_up(w):
    u = w[:, :DFF].reshape(8, 128, NFC, 128)
    g = w[:, DFF:].reshape(8, 128, NFC, 128)
    ug = np.stack([u, g], axis=3)
    return np.ascontiguousarray(ug.transpose(2, 1, 0, 3, 4).reshape(NFC, 128, 2048))


def pack_w_dn(w):
    return np.ascontiguousarray(w.reshape(NFC, 128, 1024))


def pack_w_pool(w):
    return np.ascontiguousarray(w.reshape(4, 2, 128, 256).transpose(0, 2, 1, 3).reshape(4, 128, 512))


class View:
    __slots__ = ("ap", "space", "lo", "hi", "p0", "p1")

    def __init__(self, ap, space, lo, hi, p0, p1):
        self.ap, self.space, self.lo, self.hi, self.p0, self.p1 = ap, space, lo, hi, p0, p1


class Buf:
    def __init__(self, t, space, off, ncols, es):
        self.t, self.space, self.off, self.ncols, self.es = t, space, off, ncols, es

    def __call__(self, c0, c1, p0=0, p1=128):
        assert 0 <= c0 < c1 <= self.ncols, (c0, c1, self.ncols)
        return View(self.t[p0:p1, c0:c1], self.space, self.off + c0 * self.es, self.off + c1 * self.es, p0, p1)


class Op:
    __slots__ = ("idx", "eng", "fn", "deps", "is_dma", "dsem", "dval", "signal", "sigval")


COMPUTE = ("pe", "act", "dve", "pool")


class Prog:
    def __init__(self, nc):
        self.nc = nc
        self.ops = []
        self.sb_top = SB_BASE
        nsb = (SB_END + GR) // GR
        nps = 16384 // GR
        self.lw = {"sb": np.full((2, nsb), -1, np.int64), "ps": np.full((2, nps), -1, np.int64)}
        self.lr = {
            "sb": np.full((len(COMPUTE) + 1, 2, nsb), -1, np.int64),
            "ps": np.full((len(COMPUTE) + 1, 2, nps), -1, np.int64),
        }
        self.dma_sems = {}
        self.dma_counts = {}
        self.nbuf = 0
        self.record = False

    def alloc(self, ncols, dtype, name=None):
        es = 4 if dtype == F32 else 2
        nbytes = (ncols * es + 31) // 32 * 32
        off = self.sb_top
        self.sb_top += nbytes
        assert self.sb_top <= SB_END, f"SBUF overflow {self.sb_top}"
        self.nbuf += 1
        t = self.nc.alloc_sbuf_tensor_at(f"{name or 'b'}_{self.nbuf}", [128, ncols], dtype, offset=off)
        return Buf(t, "sb", off, ncols, es)

    def alloc_at(self, off, ncols, dtype, name=None):
        es = 4 if dtype == F32 else 2
        self.nbuf += 1
        t = self.nc.alloc_sbuf_tensor_at(f"{name or 'b'}_{self.nbuf}", [128, ncols], dtype, offset=off)
        return Buf(t, "sb", off, ncols, es)

    def add(self, eng, fn, reads=(), writes=(), dma=None):
        if self.record:
            op = Op()
            op.idx = 0
            op.deps = []
            return op
        op = Op()
        op.idx = len(self.ops)
        op.eng = eng
        op.fn = fn
        op.is_dma = dma is not None
        op.dsem = dma
        op.dval = 0
        op.signal = False
        op.sigval = 0
        if op.is_dma:
            self.dma_counts[dma] = self.dma_counts.get(dma, 0) + 16
            op.dval = self.dma_counts[dma]
        strong = set()
        weak = set()
        ei = COMPUTE.index(eng) if (eng in COMPUTE and not op.is_dma) else len(COMPUTE)

        def rng(v):
            if v.space == "ps":
                return (v.lo // 2048) * (2048 // GR), ((v.hi + 2047) // 2048) * (2048 // GR)
            return v.lo // GR, (v.hi + GR - 1) // GR

        def halves(v):
            return [h for h in range(2) if (h == 0 and v.p0 < 64) or (h == 1 and v.p1 > 64)]

        for v in reads:
            g0, g1 = rng(v)
            for h in halves(v):
                strong.update(np.unique(self.lw[v.space][h, g0:g1]).tolist())
                if v.space == "ps":
                    for k in range(len(COMPUTE) + 1):
                        if k != ei:
                            strong.update(np.unique(self.lr[v.space][k, h, g0:g1]).tolist())
        for v in writes:
            g0, g1 = rng(v)
            for h in halves(v):
                strong.update(np.unique(self.lw[v.space][h, g0:g1]).tolist())
                weak.update(np.unique(self.lr[v.space][:, h, g0:g1]).tolist())
        for v in reads:
            g0, g1 = rng(v)
            for h in halves(v):
                self.lr[v.space][ei, h, g0:g1] = op.idx
        for v in writes:
            g0, g1 = rng(v)
            for h in halves(v):
                self.lw[v.space][h, g0:g1] = op.idx
                self.lr[v.space][:, h, g0:g1] = -1
        strong.discard(-1)
        weak.discard(-1)
        weak -= strong
        deps = []
        for p in strong:
            po = self.ops[p]
            if po.is_dma or op.is_dma or po.eng != eng:
                deps.append(p)
            elif eng != "pe":
                deps.append(p)
        for p in weak:
            po = self.ops[p]
            if po.is_dma or op.is_dma or po.eng != eng:
                deps.append(p)
            elif eng != "pe":
                deps.append(p)
        for p in deps:
            self.ops[p].signal = True
        op.deps = deps
        self.ops.append(op)
        return op

    def emit(self):
        nc = self.nc
        with ExitStack() as st:
            esem = {e: st.enter_context(nc.semaphore(f"s_{e}")) for e in COMPUTE}
            dsem = {k: st.enter_context(nc.semaphore(f"d_{k}")) for k in self.dma_counts}
            cnt = {e: 0 for e in COMPUTE}
            for op in self.ops:
                if op.is_dma:
                    op.signal = True
                    op.sigval = op.dval
                elif op.signal:
                    cnt[op.eng] += 1
                    op.sigval = cnt[op.eng]
            self.sig_counts = dict(cnt)
            by_eng = {e: [] for e in ("pe", "act", "dve", "pool", "sp")}
            for op in self.ops:
                by_eng[op.eng].append(op)

            def run(engname, e):
                waited = {}
                for op in by_eng[engname]:
                    need = []
                    for p in sorted(op.deps):
                        po = self.ops[p]
                        if po.is_dma:
                            sem, key, val = dsem[po.dsem], ("d", po.dsem), po.sigval
                        else:
                            sem, key, val = esem[po.eng], ("e", po.eng), po.sigval
                        if waited.get(key, 0) >= val:
                            continue
                        waited[key] = val
                        need = [w for w in need if w[1] != key]
                        need.append((sem, key, val))
                    if op.fn is None:
                        for (sem, key, val) in need:
                            e.wait_ge(sem, val)
                        continue
                    for (sem, key, val) in need[:-1]:
                        e.wait_ge(sem, val)
                    ins = op.fn(e)
                    if need:
                        ins._wait_ge(need[-1][0], need[-1][2])
                    if op.is_dma:
                        ins.then_inc(dsem[op.dsem], 16)
                    elif op.signal:
                        ins.then_inc(esem[op.eng], 1)

            with nc.Block() as block:
                @block.tensor
                def _(e):
                    run("pe", e)

                @block.scalar
                def _(e):
                    run("act", e)

                @block.vector
                def _(e):
                    run("dve", e)

                @block.gpsimd
                def _(e):
                    run("pool", e)

                @block.sync
                def _(e):
                    run("sp", e)


class Sub:
    def __init__(self, parent, base, ncols):
        self.parent, self.base, self.ncols = parent, base, ncols

    def __call__(self, c0, c1, p0=0, p1=128):
        assert 0 <= c0 < c1 <= self.ncols
        return self.parent(self.base + c0, self.base + c1, p0, p1)


class DummyBuf:
    def __call__(self, c0, c1, p0=0, p1=128):
        return View(None, "sb", 0, 0, p0, p1)


class Builder:
    def __init__(self, layers, plan=None):
        self.layers = layers
        self.plan = plan
        self.dummy = DummyBuf()
        nc = bass.Bass("TRN2", target_bir_lowering=False)
        self.nc = nc
        P = Prog(nc)
        P.record = plan is None
        self.P = P
        dt = nc.dram_tensor
        self.d_x = dt("xT", [NCH, 128, S], F32, kind="ExternalInput").ap()
        self.d_prm = dt("prm", [128, PL.n], F32, kind="ExternalInput").ap()
        self.d_cos = dt("cosT", [128, S], F32, kind="ExternalInput").ap()
        self.d_sin = dt("sinT", [128, S], F32, kind="ExternalInput").ap()
        self.d_win = dt("w_in", [2, NIN, 128, 1024], F32, kind="ExternalInput").ap()
        self.d_wout = dt("w_out", [2, 8, 128, 1024], F32, kind="ExternalInput").ap()
        self.d_wpool = dt("w_pool", [2, 4, 128, 512], F32, kind="ExternalInput").ap()
        self.d_wup = dt("w_up", [DEPTH, NFC, 128, 2048], F32, kind="ExternalInput").ap()
        self.d_wdn = dt("w_dn", [DEPTH, NFC, 128, 1024], F32, kind="ExternalInput").ap()
        self.d_y = dt("yT", [NCH, 128, S], F32, kind="ExternalOutput").ap()
        pst = nc.alloc_psum_tensor("ps", [128, 4096], F32)
        self.ps = Buf(pst, "ps", 0, 4096, 4)
        self.xT = P.alloc(NCH * S, F32, "xT")
        self.HW = S + 2
        self.hT = P.alloc(NCH * self.HW, BF16, "hT")
        self.prm = P.alloc(PL.n, F32, "prm")
        self.ones = P.alloc(128, BF16, "ones")
        self.blk = P.alloc(128, BF16, "blk")
        self.cst = P.alloc(8, F32, "cst")
        self.sqbig = P.alloc(8 * 512, BF16, "sqb")
        self.sqb = [Sub(self.sqbig, k * 512, 512) for k in range(8)]
        self.sqi = 0
        self.tmp = [P.alloc(512, F32, "tmp") for _ in range(8)]
        self.w_init()
        self.tmi = 0
        self.phase_base = P.sb_top
        self.uid = 0
        self.build()

    def bank(self, b, c0=0, c1=512, p0=0, p1=128):
        return self.ps(b * 512 + c0, b * 512 + c1, p0, p1)

    def nsq(self):
        self.sqi = (self.sqi + 1) % len(self.sqb)
        return self.sqb[self.sqi]

    def ntmp(self):
        self.tmi = (self.tmi + 1) % len(self.tmp)
        return self.tmp[self.tmi]

    def pc(self, name, c=0):
        o = PL.off[name] + c
        return self.prm(o, o + 1)

    def xt(self, c, t0, t1):
        return self.xT(c * S + t0, c * S + t1)

    def ht(self, c, t0, t1):
        return self.hT(c * self.HW + t0 + 1, c * self.HW + t1 + 1)

    def dma_key(self, base):
        self.uid += 1
        return f"{base}{self.uid}"

    def mm(self, out, lhsT, rhs, start, stop):
        self.P.add("pe", lambda e: e.matmul(out.ap, lhsT.ap, rhs.ap, start=start, stop=stop),
                   reads=[lhsT, rhs], writes=[out])

    def act(self, out, in_, func, bias=None, scale=1.0):
        reads = [in_]
        kw = {}
        if bias is not None:
            reads.append(bias)
            kw["bias"] = bias.ap
        if isinstance(scale, View):
            reads.append(scale)
            kw["scale"] = scale.ap
        else:
            kw["scale"] = float(scale)
        self.P.add("act", lambda e: e.activation(out=out.ap, in_=in_.ap, func=func, **kw),
                   reads=reads, writes=[out])

    def tt(self, eng, out, in0, in1, op):
        self.P.add(eng, lambda e: e.tensor_tensor(out=out.ap, in0=in0.ap, in1=in1.ap, op=op),
                   reads=[in0, in1], writes=[out])

    def stt(self, eng, out, in0, scalar, in1, op0, op1):
        reads = [in0, in1]
        if isinstance(scalar, View):
            reads.append(scalar)
            sc = scalar.ap
        else:
            sc = float(scalar)
        self.P.add(eng, lambda e: e.scalar_tensor_tensor(out=out.ap, in0=in0.ap, scalar=sc, in1=in1.ap, op0=op0, op1=op1),
                   reads=reads, writes=[out])

    def ts(self, eng, out, in0, s1, op0, s2=None, op1=None):
        reads = [in0]
        a1 = s1
        if isinstance(s1, View):
            reads.append(s1)
            a1 = s1.ap
        a2 = s2
        if isinstance(s2, View):
            reads.append(s2)
            a2 = s2.ap
        if op1 is None:
            self.P.add(eng, lambda e: e.tensor_scalar(out=out.ap, in0=in0.ap, scalar1=a1, scalar2=None, op0=op0),
                       reads=reads, writes=[out])
        else:
            self.P.add(eng, lambda e: e.tensor_scalar(out=out.ap, in0=in0.ap, scalar1=a1, scalar2=a2, op0=op0, op1=op1),
                       reads=reads, writes=[out])

    def recip(self, out, in_):
        self.P.add("dve", lambda e: e.reciprocal(out=out.ap, in_=in_.ap), reads=[in_], writes=[out])

    def memset(self, eng, out, val):
        self.P.add(eng, lambda e: e.memset(out.ap, val), writes=[out])

    def copy(self, eng, out, in_):
        self.P.add(eng, lambda e: e.tensor_copy(out=out.ap, in_=in_.ap), reads=[in_], writes=[out])

    def dma_in(self, queue, out, src_ap, key):
        self.P.add(queue, lambda e: e.dma_start(out=out.ap, in_=src_ap), writes=[out], dma=key)

    def dma_out(self, queue, dst_ap, in_, key):
        return self.P.add(queue, lambda e: e.dma_start(out=dst_ap, in_=in_.ap), reads=[in_], dma=key)

    WRING_BYTES = 22528
    W_AHEAD = 6
    W_NSEM = 8

    def w_init(self):
        self.wbase = self.P.alloc(self.WRING_BYTES // 2, BF16, "wring")
        self.w_reqs = []
        self.w_next_get = 0
        self.w_next_issue = 0
        self.w_head = 0
        self.w_dead = 0
        self.w_items = {}
        self.w_bufcache = {}
        self.w_semlast = {}

    def w_get(self, src_fn, ncols):
        if self.plan is None:
            self.w_reqs.append((src_fn, ncols))
            return self.dummy
        i = self.w_next_get
        self.w_next_get += 1
        assert self.plan[i][1] == ncols
        self.w_fill()
        assert i in self.w_items, "weight ring too small for live set"
        return self.w_items[i][2]

    def w_release(self):
        if self.plan is None:
            return
        self.w_dead = self.w_next_get
        for k in [k for k in self.w_items if k < self.w_dead]:
            del self.w_items[k]
        self.w_fill()

    def w_fill(self):
        while self.w_next_issue < len(self.plan) and self.w_next_issue < self.w_next_get + self.W_AHEAD:
            j = self.w_next_issue
            src_fn, ncols = self.plan[j]
            size = ncols * 2
            lo = self.w_head
            if lo + size > self.WRING_BYTES:
                lo = 0
            hi = lo + size
            if any(not (hi <= a or lo >= b) for (a, b, _, _) in self.w_items.values()):
                break
            key = (lo, ncols)
            if key not in self.w_bufcache:
                self.w_bufcache[key] = self.P.alloc_at(self.wbase.off + lo, ncols, BF16, "w")
            buf = self.w_bufcache[key]
            semk = f"w{j % self.W_NSEM}"
            op = self.P.add("pool", (lambda sf, b, n: (lambda e: e.dma_start(out=b(0, n).ap, in_=sf())))(src_fn, buf, ncols),
                            writes=[buf(0, ncols)], dma=semk)
            prev = self.w_semlast.get(semk)
            if prev is not None and prev.idx not in op.deps:
                op.deps.append(prev.idx)
            self.w_semlast[semk] = op
            self.w_items[j] = (lo, hi, buf, op)
            self.w_head = hi
            self.w_next_issue += 1

    def build(self):
        P = self.P
        self.dma_in("sp", self.prm(0, PL.n), self.d_prm, "prm")
        self.store_out = False
        self.out_ops = []
        for tt in range(4):
            for c in range(NCH):
                self.dma_in("sp", self.xt(c, tt * 512, tt * 512 + 512), self.d_x[c][:, tt * 512:tt * 512 + 512], f"x{tt}_{c}")
        self.memset("dve", self.ones(0, 128), 1.0)
        self.memset("dve", self.blk(0, 128), 0.0)
        self.memset("dve", self.blk(0, 64, 0, 64), 1.0)
        self.memset("dve", self.blk(64, 128, 64, 128), 1.0)
        self.memset("dve", self.cst(0, 1), EPS)
        for li in self.layers:
            P.sb_top = self.phase_base
            if li % 2 == 0:
                self.attn_mixer(li)
            else:
                self.pool_mixer(li)
            P.sb_top = self.phase_base
            self.ffn(li)
        fin = P.add("sp", None)
        fin.deps = [o.idx for o in self.out_ops]
        if self.plan is not None:
            P.emit()

    def x_rstd(self, t0, n, statbank, out=None):
        for c in range(NCH):
            sq = self.nsq()
            xv = self.xt(c, t0, t0 + n)
            self.act(sq(0, n), xv, AF.Square)
            self.mm(self.bank(statbank, 0, n), self.ones(0, 128), sq(0, n), c == 0, c == NCH - 1)
        return self.rsqrt_bank(statbank, n, 1.0 / D, out=out)

    def rsqrt_bank(self, statbank, n, scale, out=None):
        ln = self.ntmp()
        self.act(ln(0, n), self.bank(statbank, 0, n), AF.Ln, bias=self.cst(0, 1), scale=scale)
        if out is not None:
            self.act(out, ln(0, n), AF.Exp, scale=-0.5)
            return None
        r = self.ntmp()
        self.act(r(0, n), ln(0, n), AF.Exp, scale=-0.5)
        return r

    def prenorm_pads(self):
        for c in range(NCH):
            self.memset("pool", self.hT(c * self.HW, c * self.HW + 1), 0.0)
            self.memset("pool", self.hT(c * self.HW + S + 1, c * self.HW + S + 2), 0.0)

    def prenorm_tile(self, gname, tt):
        t0 = tt * 512
        r = self.x_rstd(t0, 512, 6 + (tt % 2))
        for c in range(NCH):
            self.stt("dve", self.ht(c, t0, t0 + 512), self.xt(c, t0, t0 + 512),
                     self.pc(gname, c), r(0, 512), ALU.mult, ALU.mult)

    def prenorm_to_hT(self, gname):
        self.prenorm_pads()
        for tt in range(4):
            self.prenorm_tile(gname, tt)

    def post_evac(self, ybank, n, d, gname, scale=None, on_act=False, comb=None):
        evv = self.ev(d * 512, d * 512 + n)
        sq = self.nsq()
        if scale is None:
            self.act(sq(0, n), self.bank(ybank, 0, n), AF.Square)
            if on_act:
                self.act(evv, self.bank(ybank, 0, n), AF.Copy, scale=self.pc(gname, d))
            else:
                self.ts("dve", evv, self.bank(ybank, 0, n), self.pc(gname, d), ALU.mult)
        else:
            self.act(sq(0, n), self.bank(ybank, 0, n), AF.Square, scale=scale)
            if on_act:
                self.act(evv, self.bank(ybank, 0, n), AF.Copy, scale=comb)
            else:
                self.ts("dve", evv, self.bank(ybank, 0, n), self.pc(gname, d), ALU.mult, scale, ALU.mult)
        return sq

    def post_apply(self, statbank, t0, n):
        r = self.rsqrt_bank(statbank, n, 1.0 / D)
        for d in range(NCH):
            evv = self.ev(d * 512, d * 512 + n)
            self.tt("dve", evv, evv, r(0, n), ALU.mult)
            xv = self.xt(d, t0, t0 + n)
            self.tt("dve", xv, xv, evv, ALU.add)
            if self.store_out:
                self.out_ops.append(self.dma_out("sp", self.d_y[d][:, t0:t0 + n], xv, f"y{t0}_{d}"))

    def pool_mixer(self, li):
        P = self.P
        o = li // 2
        PADW = 8
        W = S + 2 * PADW
        self.ev = P.alloc(8 * 512, F32, "ev")
        hf = [P.alloc(W, F32, "hf") for _ in range(2)]
        sa = [P.alloc(W, F32, "sa")]
        sb = [P.alloc(W, F32, "sb")]
        rfull = P.alloc(S, F32, "rfull")
        pgt = P.alloc(8, F32, "pgt")
        po_, go_ = PL.off[f"psc{o}"], PL.off[f"g{li}_1"]
        self.tt("dve", pgt(0, 8), self.prm(po_, po_ + 8), self.prm(go_, go_ + 8), ALU.mult)
        for b in hf:
            self.memset("pool", b(0, PADW), 0.0)
            self.memset("pool", b(PADW + S, W), 0.0)
        for tt in range(4):
            self.x_rstd(tt * 512, 512, 6 + (tt % 2), out=rfull(tt * 512, tt * 512 + 512))
        for c in range(NCH):
            gi = c // 2
            w = POOL_W[gi]
            h = hf[c % 2]
            A = sa[0]
            B = sb[0]
            for hh in range(2):
                t0 = hh * 1024
                self.stt("dve", h(PADW + t0, PADW + t0 + 1024), self.xt(c, t0, t0 + 1024), self.pc(f"g{li}_0", c),
                         rfull(t0, t0 + 1024), ALU.mult, ALU.mult)
            def lvl(dst, src, lo, hi, s0, s1, eng):
                half = (lo + hi) // 2
                for (a, b_) in ((lo, half), (half, hi)):
                    self.tt(eng, dst(a + PADW, b_ + PADW), src(a + s0 + PADW, b_ + s0 + PADW),
                            src(a + s1 + PADW, b_ + s1 + PADW), ALU.add)
            lvl(A, h, -7, S + 7, -1, 0, "dve")
            cur = A
            if w >= 4:
                lvl(B, A, -6, S + 6, -1, 1, "dve")
                cur = B
            if w >= 8:
                lvl(A, B, -4, S + 4, -2, 2, "dve")
                cur = A
            if w >= 16:
                lvl(B, A, 0, S, -4, 4, "dve")
                cur = B
            nl = w // 2
            ol = PL.off[f"pfl{gi}"]
            self.tt("dve", cur(PADW, PADW + nl), cur(PADW, PADW + nl), self.prm(ol, ol + nl), ALU.mult)
            nr = w // 2 - 1
            if nr > 0:
                orr = PL.off[f"pfr{gi}"]
                self.tt("dve", cur(PADW + S - nr, PADW + S), cur(PADW + S - nr, PADW + S), self.prm(orr, orr + nr), ALU.mult)
            for hh in range(2):
                t0 = hh * 1024
                self.stt("dve", self.ht(c, t0, t0 + 1024), cur(PADW + t0, PADW + t0 + 1024), 1.0 / w,
                         h(PADW + t0, PADW + t0 + 1024), ALU.mult, ALU.subtract)
        yb = 0
        wp = [self.w_get((lambda o_=o, g_=gi: self.d_wpool[o_, g_]), 512) for gi in range(4)]
        for tt in range(4):
            t0 = tt * 512
            sb_ = 6 + (tt % 2)
            pend = None
            for d in range(NCH):
                gi, dd = d // 2, d % 2
                b = yb % 6
                yb += 1
                for cc in range(2):
                    self.mm(self.bank(b), wp[gi](cc * 256 + dd * 128, cc * 256 + dd * 128 + 128),
                            self.ht(2 * gi + cc, t0, t0 + 512), cc == 0, cc == 1)
                sq = self.post_evac(b, 512, d, f"g{li}_1", scale=self.pc(f"psc{o}", d), on_act=True, comb=pgt(d, d + 1))
                if pend is not None:
                    self.mm(self.bank(sb_), self.ones(0, 128), pend[0](0, 512), pend[1] == 0, False)
                pend = (sq, d)
            self.mm(self.bank(sb_), self.ones(0, 128), pend[0](0, 512), False, True)
            self.post_apply(sb_, t0, 512)
        self.w_release()
        self.prenorm_to_hT(f"g{li}_2")

    def ffn(self, li):
        P = self.P
        self.store_out = (li == self.layers[-1])
        self.ev = P.alloc(8 * 512, F32, "ev")
        actb = P.alloc(NFC * 1024, BF16, "act")
        fo = PL.off[f"fcw{li}"]

        def cw(tap, ch):
            return self.prm(fo + tap * 44 + ch, fo + tap * 44 + ch + 1)

        tiles = [(0, 342), (342, 342), (684, 340)]
        pair = 0
        pend_tail = [None]
        for hf in range(2):
            base = hf * 1024
            for c in range(NFC):
                wsl = self.w_get((lambda l_=li, c_=c: self.d_wup[l_, c_]), 2048)
                for (st_, n) in tiles:
                    t0 = base + st_
                    bu = (pair % 3) * 2
                    bg = bu + 1
                    pair += 1
                    for kc in range(NCH):
                        self.mm(self.bank(bu, 0, n + 2), wsl(kc * 256, kc * 256 + 128), self.ht(kc, t0 - 1, t0 + n + 1),
                                kc == 0, kc == NCH - 1)
                    for kc in range(NCH):
                        self.mm(self.bank(bg, 0, n + 2), wsl(kc * 256 + 128, kc * 256 + 256), self.ht(kc, t0 - 1, t0 + n + 1),
                                kc == 0, kc == NCH - 1)
                    tu = self.ntmp()
                    tg = self.ntmp()
                    gl = self.ntmp()
                    self.act(tu(0, n), self.bank(bu, 1, n + 1), AF.Copy, scale=cw(1, c))
                    self.act(tg(0, n), self.bank(bg, 1, n + 1), AF.Copy, scale=cw(1, NFC + c))
                    self.stt("dve", tu(0, n), self.bank(bu, 0, n), cw(0, c), tu(0, n), ALU.mult, ALU.add)
                    self.stt("dve", tg(0, n), self.bank(bg, 0, n), cw(0, NFC + c), tg(0, n), ALU.mult, ALU.add)
                    self.stt("dve", tu(0, n), self.bank(bu, 2, n + 2), cw(2, c), tu(0, n), ALU.mult, ALU.add)
                    self.stt("dve", tg(0, n), self.bank(bg, 2, n + 2), cw(2, NFC + c), tg(0, n), ALU.mult, ALU.add)
                    if pend_tail[0] is not None:
                        pend_tail[0]()

                    def tail(gl=gl, tg=tg, tu=tu, n=n, c=c, st_=st_):
                        self.act(gl(0, n), tg(0, n), AF.Gelu_apprx_tanh)
                        self.tt("pool", actb(c * 1024 + st_, c * 1024 + st_ + n), gl(0, n), tu(0, n), ALU.mult)
                    pend_tail[0] = tail
                self.w_release()
            if pend_tail[0] is not None:
                pend_tail[0]()
                pend_tail[0] = None
            for t2 in range(2):
                a0 = t2 * 512
                t0 = base + a0
                for c in range(NFC):
                    wsl = self.w_get((lambda l_=li, c_=c: self.d_wdn[l_, c_]), 1024)
                    for d in range(NCH):
                        self.mm(self.bank(d), wsl(d * 128, d * 128 + 128), actb(c * 1024 + a0, c * 1024 + a0 + 512),
                                c == 0, c == NFC - 1)
                    self.w_release()
                prev = None
                first = True
                for d in (7, 0, 1, 2, 3, 4, 5, 6):
                    sq = self.post_evac(d, 512, d, f"g{li}_3")
                    if prev is not None:
                        self.mm(self.bank(7), self.ones(0, 128), prev(0, 512), first, False)
                        first = False
                    prev = sq
                self.mm(self.bank(7), self.ones(0, 128), prev(0, 512), False, True)
                self.post_apply(7, t0, 512)

    def attn_mixer(self, li):
        P = self.P
        e = li // 2
        self.prenorm_to_hT(f"g{li}_0")
        qT = P.alloc(4 * S, BF16, "qT")
        self.ev = P.alloc_at(qT.off, 8 * 512, F32, "ev")
        kT = P.alloc(2 * S, BF16, "kT")
        VW = 16 * 2 * 128
        vaug = P.alloc(VW, BF16, "vaug")
        convT = P.alloc(4 * S, BF16, "convT")

        def mixT(c0, c1, p0=0, p1=128):
            if c0 < 4 * S:
                kc = c0 // S
                return self.ht(kc, c0 - kc * S, c1 - kc * S)if (p0 == 0 and p1 == 128) else \
                    self.hT(kc * self.HW + 1 + (c0 - kc * S), kc * self.HW + 1 + (c1 - kc * S), p0, p1)
            return convT(c0 - 4 * S, c1 - 4 * S, p0, p1)

        z = P.alloc(S + 2, F32, "z")
        tabs = [(P.alloc(512, F32, "cos"), P.alloc(512, F32, "sin")) for _ in range(1)]

        def win_get(k):
            return self.w_get((lambda e_=e, k_=k: self.d_win[e_, k_]), 1024)
        bset = [0]

        def nb():
            b = bset[0] % 6
            bset[0] += 1
            return b

        def proj(bank, wsl, t0, n=512):
            for kc in range(NCH):
                self.mm(self.bank(bank, 0, n), wsl(kc * 128, kc * 128 + 128), self.ht(kc, t0, t0 + n), kc == 0, kc == NCH - 1)

        oq = PL.off[f"gqrow{e}"]
        ok = PL.off[f"gkrow{e}"]
        self.P.add("dve", lambda en: en.reduce_max(out=self.cst(2, 3).ap, in_=self.prm(oq, oq + 64).ap,
                                                   axis=mybir.AxisListType.X, apply_absolute_value=True),
                   reads=[self.prm(oq, oq + 64)], writes=[self.cst(2, 3)])
        self.P.add("dve", lambda en: en.reduce_max(out=self.cst(3, 4).ap, in_=self.prm(ok, ok + 64).ap,
                                                   axis=mybir.AxisListType.X, apply_absolute_value=True),
                   reads=[self.prm(ok, ok + 64)], writes=[self.cst(3, 4)])
        self.tt("dve", self.cst(1, 2), self.cst(2, 3), self.cst(3, 4), ALU.mult)
        self.ts("dve", self.cst(1, 2), self.cst(1, 2), -8.0, ALU.mult)

        so = PL.off[f"scw{e}"]
        self.memset("pool", z(0, 1), 0.0)
        self.memset("pool", z(S + 1, S + 2), 0.0)
        for c in range(4):
            wgc = win_get(10 + c)
            wci = win_get(14 + c)
            wgb = win_get(18 + c)
            for tt in range(4):
                t0 = tt * 512
                b1, b2 = nb(), nb()
                proj(b1, wgc, t0)
                proj(b2, wci, t0)
                g = self.ntmp()
                self.act(g(0, 512), self.bank(b1), AF.Copy)
                self.tt("dve", z(1 + t0, 1 + t0 + 512), self.bank(b2), g(0, 512), ALU.mult)
            for tt in range(4):
                t0 = tt * 512
                b3 = nb()
                proj(b3, wgb, t0)
                t = self.ntmp()
                self.act(t(0, 512), z(1 + t0, 1 + t0 + 512), AF.Copy, scale=self.prm(so + 4 + c, so + 4 + c + 1))
                self.stt("dve", t(0, 512), z(t0, t0 + 512), self.prm(so + c, so + c + 1), t(0, 512), ALU.mult, ALU.add)
                self.stt("dve", t(0, 512), z(t0 + 2, t0 + 514), self.prm(so + 8 + c, so + 8 + c + 1), t(0, 512), ALU.mult, ALU.add)
                self.tt("dve", mixT((4 + c) * S + t0, (4 + c) * S + t0 + 512), self.bank(b3), t(0, 512), ALU.mult)
            self.w_release()
        self.memset("pool", vaug(0, VW), 1.0)
        wv = win_get(22)
        for tk in range(16):
            b = nb()
            for kc in range(NCH):
                self.mm(self.bank(b, 0, 128), self.ht(kc, tk * 128, tk * 128 + 128),
                        wv(kc * 128, kc * 128 + 128), kc == 0, kc == NCH - 1)
            self.act(vaug((tk * 2) * 128, (tk * 2) * 128 + 64), self.bank(b, 0, 64), AF.Copy)
            self.act(vaug((tk * 2 + 1) * 128, (tk * 2 + 1) * 128 + 64), self.bank(b, 64, 128), AF.Copy)
        self.w_release()
        tabi = [0]

        def qk_chunk(wmain, wperm, gname, gpname, dst, dchunk):
            for tt in range(4):
                t0 = tt * 512
                ct, stb = tabs[0]
                tabi[0] += 1
                self.dma_in("sp", ct(0, 512), self.d_cos[:, t0:t0 + 512], "cos0")
                self.dma_in("sp", stb(0, 512), self.d_sin[:, t0:t0 + 512], "sin0")
                b1, b2, b3 = nb(), nb(), nb()
                proj(b1, wmain, t0)
                proj(b2, wperm, t0)
                sq = self.nsq()
                self.act(sq(0, 512), self.bank(b1), AF.Square)
                self.mm(self.bank(b3), self.blk(0, 128), sq(0, 512), True, True)
                t1 = self.ntmp()
                t2 = self.ntmp()
                self.act(t1(0, 512), self.bank(b1), AF.Copy, scale=self.pc(gname))
                self.act(t2(0, 512), self.bank(b2), AF.Copy, scale=self.pc(gpname))
                r = self.rsqrt_bank(b3, 512, 1.0 / 64)
                self.tt("dve", t1(0, 512), t1(0, 512), ct(0, 512), ALU.mult)
                self.tt("pool", t2(0, 512), t2(0, 512), stb(0, 512), ALU.mult)
                self.tt("pool", t1(0, 512), t1(0, 512), t2(0, 512), ALU.add)
                if dst is not None:
                    self.tt("dve", dst(dchunk * S + t0, dchunk * S + t0 + 512), t1(0, 512), r(0, 512), ALU.mult)
                else:
                    for g in range(2):
                        for oh in range(2):
                            self.tt("dve", kT(g * S + t0, g * S + t0 + 512, oh * 64, oh * 64 + 64),
                                    t1(0, 512, g * 64, g * 64 + 64), r(0, 512, g * 64, g * 64 + 64), ALU.mult)

        wm = win_get(8)
        wpm = win_get(9)
        qk_chunk(wm, wpm, f"gk{e}", f"gkp{e}", None, 0)
        self.w_release()
        for j in range(4):
            wm = win_get(j)
            wpm = win_get(4 + j)
            qk_chunk(wm, wpm, f"gq{e}", f"gqp{e}", qT, j)
            self.w_release()

        steps = [(j, qt, kc) for j in range(4) for qt in range(4) for kc in range(16)]
        LAG = 1
        ptbig = [Sub(self.sqbig, k * 1024, 1024) for k in range(3)]

        def qk_step(i):
            j, qt, kc = steps[i]
            g = j // 2
            sb0 = (i % 2) * 2
            for hh in range(2):
                p0, p1 = hh * 64, hh * 64 + 64
                self.mm(self.bank(sb0 + hh), kT(g * S + kc * 128, g * S + kc * 128 + 128, p0, p1),
                        qT(j * S + qt * 512, j * S + qt * 512 + 512, p0, p1), True, True)
            pt = ptbig[i % 3]
            self.act(pt(0, 1024), self.ps(sb0 * 512, sb0 * 512 + 1024), AF.Exp, bias=self.cst(1, 2), scale=0.125)

        def pv_step(i):
            j, qt, kc = steps[i]
            g = j // 2
            pv0 = 4 + 2 * ((i // 16) % 2)
            pt = ptbig[i % 3]
            for hh in range(2):
                self.mm(self.bank(pv0 + hh), vaug((kc * 2 + g) * 128, (kc * 2 + g) * 128 + 128),
                        pt(hh * 512, hh * 512 + 512), kc == 0, kc == 15)
            if kc == 15:
                for hh in range(2):
                    rc = self.ntmp()
                    self.recip(rc(0, 512, 64, 128), self.bank(pv0 + hh, 0, 512, 64, 128))
                    self.tt("dve", mixT(j * S + qt * 512, j * S + qt * 512 + 512, hh * 64, hh * 64 + 64),
                            self.bank(pv0 + hh, 0, 512, 0, 64), rc(0, 512, 64, 128), ALU.mult)

        for i in range(len(steps) + LAG):
            if i < len(steps):
                qk_step(i)
            if i >= LAG:
                pv_step(i - LAG)

        yb = 0
        for tt in range(4):
            t0 = tt * 512
            sb_ = 6 + (tt % 2)
            pend = None
            for d in range(NCH):
                wsl = self.w_get((lambda e_=e, d_=d: self.d_wout[e_, d_]), 1024)
                b = yb % 6
                yb += 1
                for kc in range(NCH):
                    self.mm(self.bank(b), wsl(kc * 128, kc * 128 + 128), mixT(kc * S + t0, kc * S + t0 + 512),
                            kc == 0, kc == NCH - 1)
                self.w_release()
                sq = self.post_evac(b, 512, d, f"g{li}_1", on_act=True)
                if pend is not None:
                    self.mm(self.bank(sb_), self.ones(0, 128), pend[0](0, 512), pend[1] == 0, False)
                pend = (sq, d)
            self.mm(self.bank(sb_), self.ones(0, 128), pend[0](0, 512), False, True)
            self.post_apply(sb_, t0, 512)
        self.prenorm_to_hT(f"g{li}_2")


_CACHE = {}


def get_builder(layers):
    key = tuple(layers)
    if key not in _CACHE:
        rec = Builder(list(layers), plan=None)
        _CACHE[key] = Builder(list(layers), plan=rec.w_reqs)
    return _CACHE[key]


def prep_shared(norm_g, mix_w_in, q_norm_g, k_norm_g, sconv_w, mix_w_out, pool_w, pool_scale,
                ffn_w_up, ffn_conv_w, ffn_w_down):
    cosT, sinT = rope_tables()
    return {
        "prm": pack_prm(norm_g, q_norm_g, k_norm_g, sconv_w, pool_scale, ffn_conv_w),
        "cosT": cosT,
        "sinT": sinT,
        "w_in": np.stack([pack_w_in(mix_w_in[e]) for e in range(2)]),
        "w_out": np.stack([pack_w_out(mix_w_out[e]) for e in range(2)]),
        "w_pool": np.stack([pack_w_pool(pool_w[o]) for o in range(2)]),
        "w_up": np.stack([pack_w_up(ffn_w_up[i]) for i in range(DEPTH)]),
        "w_dn": np.stack([pack_w_dn(ffn_w_down[i]) for i in range(DEPTH)]),
    }


def run_layers(xT_all, shared, layers):
    b = get_builder(layers)
    in_maps = []
    for core in range(8):
        m = dict(shared)
        m["xT"] = np.ascontiguousarray(xT_all[core])
        in_maps.append(m)
    res = run_bass_kernel_spmd(b.nc, in_maps, core_ids=list(range(8)))
    return np.stack([r["yT"] for r in res.results])


LAUNCH_GROUPS = [[0, 1, 2, 3]]


def kernel(x, norm_g, mix_w_in, q_norm_g, k_norm_g, sconv_w, mix_w_out, pool_w, pool_scale,
           ffn_w_up, ffn_conv_w, ffn_w_down):
    f = lambda a: np.asarray(a, dtype=np.float32)
    x = f(x)
    shared = prep_shared(f(norm_g), f(mix_w_in), f(q_norm_g), f(k_norm_g), f(sconv_w), f(mix_w_out),
                         f(pool_w), f(pool_scale), f(ffn_w_up), f(ffn_conv_w), f(ffn_w_down))
    xT = np.ascontiguousarray(x.transpose(0, 2, 1).reshape(8, NCH, 128, S))
    for grp in LAUNCH_GROUPS:
        xT = run_layers(xT, shared, grp)
    out = xT.reshape(8, D, S).transpose(0, 2, 1)
    return np.ascontiguousarray(out).astype(np.float32)
```
